# Optimizing a Trainium2 kernel written in Bass

```python
import jax, jax.numpy as jnp
from jax import lax
import numpy as np

D_MODEL = 1024
BATCH = 8
SEQ = 8192
DEPTH = 4
DEC_BATCH = 8
DEC_SEQ = 64
PAST_LEN = 1024

CHUNK = 64
N_META = 16
N_MIXERS = 2
N_POOL_LAYERS = (DEPTH + 1) // 2
N_RET_LAYERS = DEPTH // 2
POOL_WINDOWS = (2, 4, 8, 16)
N_POOL_GROUPS = len(POOL_WINDOWS)
POOL_GROUP = D_MODEL // N_POOL_GROUPS
POOL_HIST = max(POOL_WINDOWS) - 1
RET_HEADS = 4
RET_DK = D_MODEL // RET_HEADS
RET_DV = 2 * D_MODEL // RET_HEADS
RET_VDIM = RET_HEADS * RET_DV
ROPE_BASE = 10000.0
D_FF = 2816
EPS = 1e-6

kernel_name = "meta_pool_retention_macaron_stream_step"


def rms_norm(x, g):
    xf = x.astype(jnp.float32)
    y = xf * lax.rsqrt(jnp.mean(xf * xf, axis=-1, keepdims=True) + EPS)
    return (y * g.astype(jnp.float32)).astype(x.dtype)


def swiglu(u, w_in, w_out):
    a, b = jnp.split(u @ w_in, 2, axis=-1)
    return (jax.nn.silu(a) * b) @ w_out


def pool_mixer(u, hist, hist_valid, w_pool, scale):
    B, n, D = u.shape
    P = POOL_HIST
    full = jnp.concatenate([hist.astype(u.dtype), u], axis=1)
    new_hist = full[:, -P:]
    cs = jnp.concatenate([jnp.zeros((B, 1, D), jnp.float32),
                          jnp.cumsum(full.astype(jnp.float32), axis=1)], axis=1)
    valid = jnp.concatenate([jnp.full((P,), hist_valid, jnp.float32), jnp.ones((n,), jnp.float32)])
    cv = jnp.concatenate([jnp.zeros((1,), jnp.float32), jnp.cumsum(valid)])
    means = []
    for gi, w in enumerate(POOL_WINDOWS):
        c0, c1 = gi * POOL_GROUP, (gi + 1) * POOL_GROUP
        s = cs[:, P + 1:, c0:c1] - cs[:, P + 1 - w:P + 1 - w + n, c0:c1]
        cnt = cv[P + 1:] - cv[P + 1 - w:P + 1 - w + n]
        means.append(s / cnt[None, :, None])
    pooled = (jnp.concatenate(means, axis=-1) - u.astype(jnp.float32)).astype(u.dtype)
    pooled = pooled.reshape(B, n, N_POOL_GROUPS, POOL_GROUP)
    y = jnp.einsum('bngc,gcd->bngd', pooled, w_pool).reshape(B, n, D)
    return y * scale, new_hist


def rotary(x, pos):
    half = RET_DK // 2
    inv = ROPE_BASE ** (-jnp.arange(half, dtype=jnp.float32) / half)
    ang = pos.astype(jnp.float32)[:, None] * inv[None, :]
    cos = jnp.cos(ang)[None, :, None, :].astype(x.dtype)
    sin = jnp.sin(ang)[None, :, None, :].astype(x.dtype)
    x1, x2 = x[..., :half], x[..., half:]
    return jnp.concatenate([x1 * cos - x2 * sin, x1 * sin + x2 * cos], axis=-1)


def log_gamma():
    return jnp.log1p(-(2.0 ** (-5.0 - jnp.arange(RET_HEADS, dtype=jnp.float32))))


def retention_block(q, k, v, S):
    n = q.shape[1]
    lg = log_gamma()
    idx = jnp.arange(n, dtype=jnp.float32)
    diff = idx[:, None] - idx[None, :]
    dmask = jnp.where(diff[None] >= 0,
                      jnp.exp(jnp.maximum(diff, 0.0)[None] * lg[:, None, None]),
                      0.0).astype(q.dtype)
    scores = jnp.einsum('bihd,bjhd->bhij', q, k) * dmask[None]
    intra = jnp.einsum('bhij,bjhe->bihe', scores, v)
    qdec = jnp.exp((idx + 1.0)[:, None] * lg[None, :]).astype(q.dtype)
    cross = jnp.einsum('bihd,bhde->bihe', q * qdec[None, :, :, None], S)
    kdec = jnp.exp((n - 1.0 - idx)[:, None] * lg[None, :]).astype(q.dtype)
    s_dec = jnp.exp(n * lg).astype(S.dtype)[None, :, None, None]
    S_new = s_dec * S + jnp.einsum('bjhd,bjhe->bhde', k * kdec[None, :, :, None], v)
    return intra + cross, S_new


def retention_mixer(u, pos, S0, w_in, w_out, norm_g):
    B, n, D = u.shape
    proj = u @ w_in
    q, k, v, g = jnp.split(proj, [D, 2 * D, 2 * D + RET_VDIM], axis=-1)
    q = rotary(q.reshape(B, n, RET_HEADS, RET_DK), pos)
    k = rotary(k.reshape(B, n, RET_HEADS, RET_DK), pos) * (RET_DK ** -0.5)
    v = v.reshape(B, n, RET_HEADS, RET_DV)
    if S0 is None:
        pad = (-n) % CHUNK
        nc = (n + pad) // CHUNK

        def to_blocks(t):
            t = jnp.pad(t, ((0, 0), (pad, 0), (0, 0), (0, 0)))
            return jnp.moveaxis(t.reshape(B, nc, CHUNK, RET_HEADS, t.shape[-1]), 1, 0)

        def step(S, qkv):
            o, S2 = retention_block(qkv[0], qkv[1], qkv[2], S)
            return S2, o

        S_init = jnp.zeros((B, RET_HEADS, RET_DK, RET_DV), q.dtype)
        S_fin, o = lax.scan(step, S_init, (to_blocks(q), to_blocks(k), to_blocks(v)))
        o = jnp.moveaxis(o, 0, 1).reshape(B, nc * CHUNK, RET_HEADS, RET_DV)[:, pad:]
    else:
        o, S_fin = retention_block(q, k, v, S0.astype(q.dtype))
    of = o.astype(jnp.float32)
    of = of * lax.rsqrt(jnp.mean(of * of, axis=-1, keepdims=True) + EPS)
    o = (of.reshape(B, n, RET_VDIM) * norm_g.astype(jnp.float32)).astype(u.dtype)
    y = (jax.nn.silu(g) * o) @ w_out
    return y, S_fin


def setup_inputs(seed: int = 0) -> dict:
    key = jax.random.key(seed)
    ks = jax.random.split(key, 20)
    f32 = jnp.float32
    nrm = lambda k, shape, s: jax.random.normal(k, shape, f32) * s
    return {
        "x_prompt": nrm(ks[0], (BATCH, SEQ, D_MODEL), 1.0),
        "x_sample": nrm(ks[1], (DEC_BATCH, DEC_SEQ, D_MODEL), 1.0),
        "cache_pool": nrm(ks[2], (N_POOL_LAYERS, DEC_BATCH, POOL_HIST, D_MODEL), 1.0),
        "state_ret": nrm(ks[3], (N_RET_LAYERS, DEC_BATCH, RET_HEADS, RET_DK, RET_DV), 0.5),
        "meta_tokens": nrm(ks[4], (N_META, D_MODEL), 1.0),
        "norm_g": 1.0 + nrm(ks[5], (DEPTH, 3, D_MODEL), 0.02),
        "final_norm_g": 1.0 + nrm(ks[6], (D_MODEL,), 0.02),
        "w_ffn1_in": nrm(ks[7], (DEPTH, D_MODEL, 2 * D_FF), D_MODEL ** -0.5),
        "w_ffn1_out": nrm(ks[8], (DEPTH, D_FF, D_MODEL), D_FF ** -0.5),
        "w_ffn2_in": nrm(ks[9], (DEPTH, D_MODEL, 2 * D_FF), D_MODEL ** -0.5),
        "w_ffn2_out": nrm(ks[10], (DEPTH, D_FF, D_MODEL), D_FF ** -0.5),
        "w_pool": nrm(ks[11], (N_POOL_LAYERS, N_POOL_GROUPS, POOL_GROUP, POOL_GROUP), POOL_GROUP ** -0.5),
        "pool_scale": 1.0 + nrm(ks[12], (N_POOL_LAYERS, D_MODEL), 0.02),
        "w_ret_in": nrm(ks[13], (N_RET_LAYERS, D_MODEL, 2 * D_MODEL + 2 * RET_VDIM), D_MODEL ** -0.5),
        "w_ret_out": nrm(ks[14], (N_RET_LAYERS, RET_VDIM, D_MODEL), RET_VDIM ** -0.5),
        "ret_norm_g": 1.0 + nrm(ks[15], (N_RET_LAYERS, RET_VDIM), 0.02),
    }


def reference(x_prompt, x_sample, cache_pool, state_ret, meta_tokens, norm_g, final_norm_g,
              w_ffn1_in, w_ffn1_out, w_ffn2_in, w_ffn2_out, w_pool, pool_scale,
              w_ret_in, w_ret_out, ret_norm_g):
    B = x_prompt.shape[0]
    meta = jnp.broadcast_to(meta_tokens[None].astype(x_prompt.dtype), (B, N_META, D_MODEL))
    xp = jnp.concatenate([meta, x_prompt], axis=1)
    xs = x_sample
    pos_p = jnp.arange(xp.shape[1])
    pos_s = N_META + PAST_LEN + jnp.arange(xs.shape[1])
    pool_p, pool_s, ret_p, ret_s = [], [], [], []
    for i in range(DEPTH):
        g = norm_g[i]
        xp = xp + 0.5 * swiglu(rms_norm(xp, g[0]), w_ffn1_in[i], w_ffn1_out[i])
        xs = xs + 0.5 * swiglu(rms_norm(xs, g[0]), w_ffn1_in[i], w_ffn1_out[i])
        up = rms_norm(xp, g[1])
        us = rms_norm(xs, g[1])
        j = i // N_MIXERS
        if i % N_MIXERS == 0:
            hist0 = jnp.zeros((B, POOL_HIST, D_MODEL), up.dtype)
            mp, hp = pool_mixer(up, hist0, 0.0, w_pool[j], pool_scale[j])
            ms, hs = pool_mixer(us, cache_pool[j], 1.0, w_pool[j], pool_scale[j])
            pool_p.append(hp)
            pool_s.append(hs)
        else:
            mp, sp = retention_mixer(up, pos_p, None, w_ret_in[j], w_ret_out[j], ret_norm_g[j])
            ms, ss = retention_mixer(us, pos_s, state_ret[j], w_ret_in[j], w_ret_out[j], ret_norm_g[j])
            ret_p.append(sp)
            ret_s.append(ss)
        xp = xp + mp
        xs = xs + ms
        xp = xp + 0.5 * swiglu(rms_norm(xp, g[2]), w_ffn2_in[i], w_ffn2_out[i])
        xs = xs + 0.5 * swiglu(rms_norm(xs, g[2]), w_ffn2_in[i], w_ffn2_out[i])
    y_prompt = rms_norm(xp, final_norm_g)[:, N_META:]
    y_sample = rms_norm(xs, final_norm_g)
    return (y_prompt, y_sample, jnp.stack(pool_p), jnp.stack(pool_s), jnp.stack(ret_p), jnp.stack(ret_s))
```

```python
import contextlib
import numpy as np
import concourse.bass as bass
import concourse.mybir as mybir
from concourse.bass_utils import run_bass_kernel_spmd

F32 = mybir.dt.float32
BF16 = mybir.dt.bfloat16
ALU = mybir.AluOpType
AF = mybir.ActivationFunctionType

D = 1024
SEQ = 8192
NMETA = 16
DEC_SEQ = 64
PAST = 1024
DFF = 2816
NJ = DFF // 128
EPS = 1e-6
TM = 256
WIN = (2, 4, 8, 16)
GAM = [1.0 - 2.0 ** (-5.0 - h) for h in range(4)]
SLOT_U = 40
NSLOT = 4
U_FFN_IN = 352
U_FFN_OUT = 176
U_POOL = 16
U_RET = 512
U_TILE = 4 * 2 * (U_FFN_IN + U_FFN_OUT) + 2 * U_POOL + 2 * U_RET
NCHUNK = U_TILE // SLOT_U
NPOS = NMETA + SEQ + DEC_SEQ

ENG_ATTR = {"pe": "tensor", "act": "scalar", "dve": "vector", "pool": "gpsimd", "sp": "sync"}


class Rec:
    def __init__(self):
        self.ops = {e: [] for e in ENG_ATTR}
        self.cnt = {}
        self.known = {e: {} for e in ENG_ATTR}
        self.buf = {}
        self.final_waits = {}

    def _st(self, k):
        s = self.buf.get(k)
        if s is None:
            s = {"w": None, "r": []}
            self.buf[k] = s
        return s

    def op(self, eng, fn, reads=(), writes=(), sig=True, dma_sem=None):
        waits = {}
        idx = len(self.ops[eng])

        def need(tok):
            sem, val, pidx = tok
            if sem == eng:
                if eng == "pe":
                    return
                if pidx < idx - 2:
                    return
            if self.known[eng].get(sem, 0) >= val:
                return
            if waits.get(sem, 0) < val:
                waits[sem] = val

        for k in reads:
            s = self._st(k)
            if s["w"] is not None:
                need(s["w"])
        for k in writes:
            s = self._st(k)
            if s["w"] is not None:
                need(s["w"])
            for t in s["r"]:
                need(t)
        for sem, val in waits.items():
            self.known[eng][sem] = val
        if dma_sem is not None:
            self.cnt[dma_sem] = self.cnt.get(dma_sem, 0) + 16
            tok = (dma_sem, self.cnt[dma_sem], idx)
            inc = (dma_sem, 16)
        elif sig:
            self.cnt[eng] = self.cnt.get(eng, 0) + 1
            tok = (eng, self.cnt[eng], idx)
            inc = (eng, 1)
        else:
            tok = (eng, self.cnt.get(eng, 0) + 1, idx)
            inc = None
        for k in reads:
            if k in writes:
                continue
            self._st(k)["r"].append(tok)
        for k in writes:
            s = self._st(k)
            s["w"] = tok
            s["r"] = []
        self.ops[eng].append((list(waits.items()), fn, inc))
        return tok


def build_program():
    nc = bass.Bass("TRN2", target_bir_lowering=False)
    R = Rec()

    def din(name, shape, dt=F32):
        return nc.dram_tensor(name, list(shape), dt, kind="ExternalInput").ap()

    def dout(name, shape):
        return nc.dram_tensor(name, list(shape), F32, kind="ExternalOutput").ap()

    xp_d = din("xp", [SEQ, D])
    xs_d = din("xs", [DEC_SEQ, D])
    cpool_d = din("cpool", [2, 15, D])
    sret_d = din("sret", [2, 4, 256, 512])
    meta_d = din("meta", [NMETA, D])
    gn_d = din("gn", [128, 104])
    pscale_d = din("pscale", [128, 16])
    rgn_d = din("rgn", [128, 32])
    wf_in_d = [din("w_ffn1_in", [4, D, 2 * DFF]), din("w_ffn2_in", [4, D, 2 * DFF])]
    wf_out_d = [din("w_ffn1_out", [4, DFF, D]), din("w_ffn2_out", [4, DFF, D])]
    wpool_d = din("w_pool", [2, 4, 256, 256])
    wri_d = din("w_ret_in", [2, D, 6144])
    wro_d = din("w_ret_out", [2, 2048, D])
    cos_d = din("rope_cos", [128, NPOS])
    sin_d = din("rope_sin", [128, NPOS])
    mask_d = din("cmask", [128, 384])
    qdec_d = din("qdec", [128, 4 * TM])
    kdec_d = din("kdec", [128, 4 * TM])
    invc_d = din("invcnt", [128, 64])
    ident_d = din("ident", [128, 128])
    ones_d = din("ones", [128, 128])

    yp_d = dout("yp", [SEQ, D])
    ys_d = dout("ys", [DEC_SEQ, D])
    npp_d = dout("npp", [2, 15, D])
    nps_d = dout("nps", [2, 15, D])
    nrp_d = dout("nrp", [2, 4, 256, 512])
    nrs_d = dout("nrs", [2, 4, 256, 512])

    wstream = nc.dram_tensor("wstream", [128, U_TILE * 128], BF16, kind="Internal").ap()

    es = contextlib.ExitStack()
    with es:
        def sb(name, shape, dt=F32):
            return es.enter_context(nc.sbuf_tensor(name, list(shape), dt))

        def pst(name, shape, dt=F32):
            return es.enter_context(nc.psum_tensor(name, list(shape), dt))

        xT = sb("xT", [128, 8, TM])
        S = [sb("S0", [128, 8, 512]), sb("S1", [128, 8, 512])]
        Sbf = sb("Sbf", [128, 8, 512], BF16)
        hist = [sb("hist0", [128, 8, 15]), sb("hist1", [128, 8, 15])]
        ident_f = sb("ident_f", [128, 128])
        ident_b = sb("ident_b", [128, 128], BF16)
        ones_b = sb("ones_b", [128, 128], BF16)
        mask = sb("mask", [128, 384])
        qdec = sb("qdec_s", [128, 4, TM])
        kdec = sb("kdec_s", [128, 4, TM])
        invc = sb("invc", [128, 4, 16])
        gn = sb("gn_s", [128, 104])
        pscale = sb("pscale_s", [128, 16])
        rgn = sb("rgn_s", [128, 32])
        uT = sb("uT", [128, 8, TM], BF16)
        hT = sb("hT", [128, NJ, TM], BF16)
        sa = [sb(f"sa{i}", [128, TM]) for i in range(2)]
        abr = [sb(f"abr{i}", [128, 2, TM]) for i in range(2)]
        negh = sb("negh", [128, TM])
        csr = sb("csr", [128, 2, TM])
        rtm = [sb(f"rtm{i}", [128, 2]) for i in range(2)]
        dsq = sb("dsq", [128, 1])
        rs = [sb(f"rs{i}", [128, TM]) for i in range(2)]
        uf = sb("uf", [128, 8, 15 + TM])
        tmpw = [sb(f"tmpw{i}", [128, 2, 15 + TM]) for i in range(2)]
        qt = sb("qt", [128, 8, TM], BF16)
        kt = sb("kt", [128, 8, TM], BF16)
        ktm = sb("ktm", [128, 2, 1024], BF16)
        vt = sb("vt", [128, 2, 2048], BF16)
        smT = sb("smT", [128, 4, 2, TM], BF16)
        cs = [sb(f"cs{i}", [128, 2, TM]) for i in range(2)]
        xsb = [sb(f"xsb{i}", [128, 2, TM]) for i in range(2)]
        t1b = [sb(f"t1b{i}", [128, 2, TM]) for i in range(2)]
        t2b = [sb(f"t2b{i}", [128, 2, TM]) for i in range(2)]
        osq = [sb(f"osq{i}", [128, 4, TM], BF16) for i in range(2)]
        xsq = sb("xsq", [128, 8, TM], BF16)
        ntmp = [sb(f"ntmp{i}", [128, TM]) for i in range(2)]
        tmpd = [sb(f"tmpd{i}", [128, 2, 15 + TM]) for i in range(2)]
        rsh = [sb(f"rsh{i}", [128, TM]) for i in range(2)]
        xin = sb("xin", [128, 2, D])
        yout = sb("yout", [128, D])
        wsl = [sb(f"wsl{i}", [128, SLOT_U * 128], BF16) for i in range(NSLOT)]
        psb = [pst(f"ps{i}", [128, 512]) for i in range(7)]
        pbf = pst("pbf", [128, 1024], BF16)

        sem_names = list(ENG_ATTR) + ["cvt", "cst", "xi", "csl0", "csl1", "yo0", "yo1", "so0", "so1", "ho0", "ho1", "sl", "hl"] + [f"wl{i}" for i in range(NSLOT)]
        sems = {n: es.enter_context(nc.semaphore(n)) for n in sem_names}

        st = {"ps": 0, "sa": 0, "rs": 0, "xs": 0, "rsh": 0, "wpos": 0, "wload": 0, "nt": 0, "abr": 0, "rtm": 0}

        def ps_next():
            i = st["ps"]
            st["ps"] = (i + 1) % 7
            return psb[i], f"ps{i}"

        def mm(out, lhsT, rhs, start, stop, reads, writes, sig=None):
            if sig is None:
                sig = stop
            R.op("pe", lambda e, o=out, l=lhsT, r=rhs, s0=start, s1=stop: e.matmul(o, l, r, start=s0, stop=s1),
                 reads=reads, writes=writes, sig=sig)

        def tr(out, in_, ident, reads, writes, sig):
            R.op("pe", lambda e, o=out, i=in_, d=ident: e.transpose(o, i, d), reads=reads, writes=writes, sig=sig)

        def act(func, out, in_, reads, writes, scale=1.0, bias=0.0):
            R.op("act", lambda e, o=out, i=in_, f=func, s=scale, b=bias: e.activation(out=o, in_=i, func=f, bias=b, scale=s),
                 reads=reads, writes=writes)

        def tt(eng, out, in0, in1, op, reads, writes):
            R.op(eng, lambda e, o=out, a=in0, b=in1, p=op: e.tensor_tensor(o, a, b, p), reads=reads, writes=writes)

        def stt(eng, out, in0, scalar, in1, op0, op1, reads, writes):
            R.op(eng, lambda e, o=out, a=in0, s=scalar, b=in1, p0=op0, p1=op1: e.scalar_tensor_tensor(o, a, s, b, p0, p1),
                 reads=reads, writes=writes)

        def cp(eng, out, in_, reads, writes):
            R.op(eng, lambda e, o=out, i=in_: e.tensor_copy(o, i), reads=reads, writes=writes)

        def dma(eng, out, in_, reads, writes, sem, slow=False):
            if slow:
                fn = lambda e, o=out, i=in_: e.dma_start(out=o, in_=i, allow_slow_non_contiguous=True)
            else:
                fn = lambda e, o=out, i=in_: e.dma_start(out=o, in_=i)
            R.op(eng, fn, reads=reads, writes=writes, dma_sem=sem)

        def wload_upto(gchunk):
            while st["wload"] <= gchunk:
                g = st["wload"]
                c = g % NCHUNK
                s = g % NSLOT
                dma("sp", wsl[s][:, :], wstream[:, c * SLOT_U * 128:(c + 1) * SLOT_U * 128],
                    reads=["wstream"], writes=[f"w{s}"], sem=f"wl{s}")
                st["wload"] = g + 1

        def wnext(n=1):
            pos = st["wpos"]
            g = pos // SLOT_U
            off = pos % SLOT_U
            assert off + n <= SLOT_U
            wload_upto(g + NSLOT - 1)
            st["wpos"] = pos + n
            s = g % NSLOT
            return wsl[s][:, off * 128:(off + n) * 128], f"w{s}"

        def b3(ap2, T):
            return ap2.rearrange("p (a t) -> p a t", a=2)

        def bc2(ap, T):
            return ap.unsqueeze(1).broadcast_to([ap.shape[0], 2, T])

        def cdma(out, in_, key, eng="sp"):
            dma(eng, out, in_, reads=[], writes=[key], sem="cst")

        cdma(ident_f[:, :], ident_d[:, :], "ident_f")
        cdma(mask[:, :], mask_d[:, :], "mask")
        cdma(qdec[:, :, :], qdec_d.rearrange("p (h t) -> p h t", h=4), "qdec")
        cdma(kdec[:, :, :], kdec_d.rearrange("p (h t) -> p h t", h=4), "kdec")
        cdma(invc[:, :, :], invc_d.rearrange("p (g t) -> p g t", g=4), "invc")
        cdma(gn[:, :], gn_d[:, :], "gn")
        cdma(pscale[:, :], pscale_d[:, :], "pscale")
        cdma(rgn[:, :], rgn_d[:, :], "rgn")
        for k in ["ident_f", "mask", "qdec", "kdec", "invc", "gn", "pscale", "rgn"]:
            R._st(k)["w"] = ("cst", R.cnt["cst"], 0)
        R.op("pool", lambda e, o=negh[:, :]: e.memset(o, -0.5), reads=[], writes=["negh"])
        ncvt = [0]

        def cvt(out, in_):
            R.ops["pool"].append(([], (lambda e, o=out, i=in_: e.dma_start(out=o, in_=i)), ("cvt", 16)))
            ncvt[0] += 1

        cvt(ident_b[:, :], ident_d[:, :])
        cvt(ones_b[:, :], ones_d[:, :])
        upos = 0

        def wsview(u0, n):
            return wstream[:, u0 * 128:(u0 + n) * 128]

        def cvt_cols(w2d, c0, nkc):
            nonlocal upos
            src = w2d[:, c0:c0 + 128].rearrange("(kc p) c -> p kc c", p=128)
            cvt(wsview(upos, nkc).rearrange("p (kc c) -> p kc c", kc=nkc), src)
            upos += nkc

        for l in range(4):
            for which in range(2):
                if which == 1:
                    j = l // 2
                    if l % 2 == 0:
                        for g in range(4):
                            for dch in range(2):
                                cvt_cols(wpool_d[j, g], dch * 128, 2)
                    else:
                        w = wri_d[j]
                        for oc in range(16):
                            cvt_cols(w, oc * 128, 8)
                        for eb in range(4):
                            c0 = 2048 + eb * 512
                            src = w[:, c0:c0 + 512].rearrange("(kc p) c -> p kc c", p=128)
                            cvt(wsview(upos, 32).rearrange("p (kc c) -> p kc c", kc=8), src)
                            upos += 32
                        for oc in range(16):
                            cvt_cols(w, 4096 + oc * 128, 8)
                        wo = wro_d[j]
                        for m in range(8):
                            cvt_cols(wo, m * 128, 16)
                wi = wf_in_d[which][l]
                for j2 in range(NJ):
                    for part in range(2):
                        cvt_cols(wi, part * DFF + j2 * 128, 8)
                wo = wf_out_d[which][l]
                for m in range(8):
                    cvt_cols(wo, m * 128, NJ)
        assert upos == U_TILE, upos
        R.cnt["cvt"] = 16 * ncvt[0]
        tokc = ("cvt", R.cnt["cvt"], 0)
        for k in ["wstream", "ident_b", "ones_b"]:
            R._st(k)["w"] = tokc

        def squares(T, no_act=False):
            for kp in range(4):
                xs2 = xT[:, 2 * kp:2 * kp + 2, :T]
                if kp % 2 == 1 and no_act:
                    tt("dve", xsq[:, 2 * kp:2 * kp + 2, :T], xs2, xs2, ALU.mult,
                       reads=[f"x{2 * kp}", f"x{2 * kp + 1}"], writes=[f"xq{2 * kp}", f"xq{2 * kp + 1}"])
                elif kp % 2 == 1:
                    act(AF.Square, xsq[:, 2 * kp:2 * kp + 2, :T], xs2, reads=[f"x{2 * kp}", f"x{2 * kp + 1}"],
                        writes=[f"xq{2 * kp}", f"xq{2 * kp + 1}"])
                else:
                    tt("pool", xsq[:, 2 * kp:2 * kp + 2, :T], xs2, xs2, ALU.mult,
                       reads=[f"x{2 * kp}", f"x{2 * kp + 1}"], writes=[f"xq{2 * kp}", f"xq{2 * kp + 1}"])

        def rstd_from(bank, bk, T, rbuf, rk, inv_n):
            act(AF.Sqrt, rbuf[:, :T], bank[:, :T], reads=[], writes=[bk, rk], scale=inv_n, bias=EPS)
            R.op("dve", lambda e, o=rbuf[:, :T]: e.reciprocal(o, o), reads=[], writes=[rk])

        def sumsq_rstd(T):
            bank, bk = ps_next()
            for kc in range(8):
                mm(bank[:, :T], ones_b[:, :], xsq[:, kc, :T], kc == 0, kc == 7, reads=["ones_b", f"xq{kc}"], writes=[bk])
            i = st["rs"]
            st["rs"] = (i + 1) % 2
            rstd_from(bank, bk, T, rs[i], f"rs{i}", 1.0 / D)
            return rs[i], f"rs{i}"

        def rmsnorm(gcol, T, dst):
            act(AF.Sqrt, dsq[:, 0:1], gn[:, 0:1], reads=["gn"], writes=["dsq"])
            squares(T)
            r, rk = sumsq_rstd(T)
            for kc in range(8):
                if dst == "u":
                    o = uT[:, kc, :T]
                    wk = f"u{kc}"
                else:
                    o = uf[:, kc, 15:15 + T]
                    wk = f"uf{kc}"
                gcolap = gn[:, gcol + kc:gcol + kc + 1]
                if kc % 2 == 0:
                    stt("dve", o, xT[:, kc, :T], gcolap, r[:, :T], ALU.mult, ALU.mult,
                        reads=[f"x{kc}", rk, "gn"], writes=[wk])
                else:
                    ti_ = st["nt"]
                    st["nt"] = (ti_ + 1) % 2
                    tt("pool", ntmp[ti_][:, :T], xT[:, kc, :T], r[:, :T], ALU.mult, reads=[f"x{kc}", rk], writes=[f"ntmp{ti_}"])
                    act(AF.Copy, o, ntmp[ti_][:, :T], reads=[f"ntmp{ti_}", "gn"], writes=[wk], scale=gcolap)

        def emit_xg(chunks, gcol, T):
            if gcol is None:
                return
            for kc in chunks:
                gcolap = gn[:, gcol + kc:gcol + kc + 1]
                if gcol == 96:
                    o, wk = uf[:, kc, 15:15 + T], f"uf{kc}"
                else:
                    o, wk = uT[:, kc, :T], f"u{kc}"
                if kc % 2 == 0:
                    R.op("dve", lambda e, o=o, a_=xT[:, kc, :T], g_=gcolap: e.tensor_scalar(o, a_, g_, None, ALU.mult),
                         reads=[f"x{kc}", "gn"], writes=[wk])
                else:
                    act(AF.Copy, o, xT[:, kc, :T], reads=[f"x{kc}", "gn"], writes=[wk], scale=gcolap)

        def rstd_tm(T):
            rows = min(T, 128)
            nb = (T + 127) // 128
            bank, bk = ps_next()
            for tb in range(nb):
                for kc in range(8):
                    mm(bank[:rows, tb:tb + 1], xsq[:, kc, tb * 128:tb * 128 + rows], ones_b[:, 0:1], kc == 0, kc == 7,
                       reads=["ones_b", f"xq{kc}"], writes=[bk])
            i = st["rtm"]
            st["rtm"] = (i + 1) % 2
            rt = rtm[i]
            act(AF.Sqrt, rt[:rows, :nb], bank[:rows, :nb], reads=[], writes=[bk, f"rtm{i}"], scale=1.0 / D, bias=EPS)
            R.op("dve", lambda e, o=rt[:rows, :nb]: e.reciprocal(o, o), reads=[], writes=[f"rtm{i}"])
            return rt, f"rtm{i}"

        def ffn(l, which, T, next_g):
            squares(T, no_act=(l == 0 and which == 0))
            r = rk = None
            pend = []
            for j in range(NJ):
                bank, bk = ps_next()
                for part in range(2):
                    for kc in range(8):
                        w, wk = wnext()
                        mm(bank[:, part * T:(part + 1) * T], w, uT[:, kc, :T], kc == 0, kc == 7, reads=[wk, f"u{kc}"], writes=[bk])
                pend.append((j, bank, bk))
                if j == 0:
                    continue
                if j == 1:
                    r, rk = sumsq_rstd(T)
                for (jj, bank_, bk_) in pend:
                    ia = st["abr"]
                    st["abr"] = (ia + 1) % 2
                    ab = abr[ia]
                    tt("dve", ab[:, :, :T], b3(bank_[:, :2 * T], T), bc2(r[:, :T], T), ALU.mult, reads=[rk], writes=[bk_, f"abr{ia}"])
                    i = st["sa"]
                    st["sa"] = (i + 1) % 2
                    act(AF.Silu, sa[i][:, :T], ab[:, 0, :T], reads=[f"abr{ia}"], writes=[f"sa{i}"])
                    tt("pool", hT[:, jj, :T], sa[i][:, :T], ab[:, 1, :T], ALU.mult, reads=[f"sa{i}", f"abr{ia}"], writes=[f"h{jj}"])
                pend = []
                ps_ = st.get("pending_sbf")
                if ps_ is not None and j >= 2:
                    jo, ci = ps_
                    act(AF.Copy, Sbf[:, ci, :], S[jo][:, ci, :], reads=[f"S{jo}_{ci}"], writes=[f"Sbf{ci}"])
                    st["pending_sbf"] = [jo, ci + 1] if ci < 7 else None
            for mp in range(4):
                bank, bk = ps_next()
                for mi in range(2):
                    for kc in range(NJ):
                        w, wk = wnext()
                        mm(bank[:, mi * T:(mi + 1) * T], w, hT[:, kc, :T], kc == 0, kc == NJ - 1, reads=[wk, f"h{kc}"], writes=[bk])
                xs_ = xT[:, 2 * mp:2 * mp + 2, :T]
                stt("dve", xs_, b3(bank[:, :2 * T], T), 0.5, xs_, ALU.mult, ALU.add,
                    reads=[], writes=[bk, f"x{2 * mp}", f"x{2 * mp + 1}"])
                emit_xg([2 * mp, 2 * mp + 1], next_g, T)

        def pool_layer(l, T, first_prompt, last, stream, next_g):
            j = l // 2
            L = 15 + T
            ufk = [f"uf{k}" for k in range(8)]
            cp("pool", uf[:, :, 0:15], hist[j][:, :, :], reads=[f"hist{j}"], writes=ufk)
            rmsnorm((l * 3 + 1) * 8, T, "uf")
            for g in (0, 1, 2, 3):
                c0 = 2 * g
                weng = "pool" if g == 3 else "dve"
                tbufs = tmpw if g == 3 else tmpd
                tname = "tmpw" if g == 3 else "tmpd"
                cur = uf[:, c0:c0 + 2, :]
                curk = [f"uf{c0}", f"uf{c0 + 1}"]
                for d in range(g + 1):
                    s = 1 << d
                    v0 = (1 << (d + 1)) - 1
                    dstt = tbufs[d % 2]
                    tt(weng, dstt[:, :, v0:L], cur[:, :, v0:L], cur[:, :, v0 - s:L - s], ALU.add,
                       reads=curk, writes=[f"{tname}{d % 2}"])
                    cur = dstt
                    curk = [f"{tname}{d % 2}"]
                o = uT[:, c0:c0 + 2, :T]
                ok = [f"u{c0}", f"u{c0 + 1}"]
                if first_prompt:
                    other = tbufs[(g + 1) % 2]
                    otherk = f"{tname}{(g + 1) % 2}"
                    tt(weng, other[:, :, 15:15 + T], cur[:, :, 15:15 + T], bc2(invc[:, g, :T], T), ALU.mult,
                       reads=curk + ["invc"], writes=[otherk])
                    tt(weng, o, other[:, :, 15:15 + T], uf[:, c0:c0 + 2, 15:15 + T], ALU.subtract,
                       reads=[otherk, f"uf{c0}", f"uf{c0 + 1}"], writes=ok)
                else:
                    stt("dve", o, cur[:, :, 15:15 + T], 1.0 / WIN[g], uf[:, c0:c0 + 2, 15:15 + T], ALU.mult, ALU.subtract,
                        reads=curk + [f"uf{c0}", f"uf{c0 + 1}"], writes=ok)
            for g in range(4):
                bank, bk = ps_next()
                for dch in range(2):
                    for cc in range(2):
                        w, wk = wnext()
                        mm(bank[:, dch * T:(dch + 1) * T], w, uT[:, 2 * g + cc, :T], cc == 0, cc == 1, reads=[wk, f"u{2 * g + cc}"], writes=[bk])
                for dch in range(2):
                    c = 2 * g + dch
                    stt("dve", xT[:, c, :T], bank[:, dch * T:(dch + 1) * T], pscale[:, j * 8 + c:j * 8 + c + 1], xT[:, c, :T],
                        ALU.mult, ALU.add, reads=["pscale"], writes=[bk, f"x{c}"])
                emit_xg([2 * g, 2 * g + 1], next_g, T)
            cp("pool", hist[j][:, :, :], uf[:, :, T:T + 15], reads=ufk, writes=[f"hist{j}"])
            if last:
                dstd = (npp_d if stream == "p" else nps_d)[j]
                for kc in range(8):
                    dma("sp", dstd[:, kc * 128:(kc + 1) * 128].rearrange("t p -> p t"), hist[j][:, kc, :],
                        reads=[f"hist{j}"], writes=[], sem=f"ho{j}", slow=True)

        def ret_layer(l, T, csbuf, last, stream, next_g):
            j = l // 2
            Sj = S[j]
            rows = min(T, 128)
            nb = (T + 127) // 128
            if st.get("sbf") != j:
                for i in range(8):
                    act(AF.Copy, Sbf[:, i, :], Sj[:, i, :], reads=[f"S{j}_{i}"], writes=[f"Sbf{i}"])
                st["sbf"] = j
            squares(T)
            cst = cs[csbuf]
            csk = f"cs{csbuf}"
            r = rk = rt = rtk = None

            def rot_evac(h, bank, bk, dec, deck, dstT, dk):
                i = st["xs"]
                st["xs"] = (i + 1) % 2
                xs_, t1, t2 = xsb[i], t1b[i], t2b[i]
                tt("dve", xs_[:, :, :T], b3(bank[:, :2 * T], T), bc2(dec[:, h, :T], T), ALU.mult,
                   reads=[deck], writes=[bk, f"xsb{i}"])
                tt("pool", t1[:, :, :T], xs_[:, :, :T], bc2(csr[:, 0, :T], T), ALU.mult, reads=[f"xsb{i}", "csr"], writes=[f"t1b{i}"])
                tt("pool", t2[:, :, :T], xs_[:, :, :T], bc2(csr[:, 1, :T], T), ALU.mult, reads=[f"xsb{i}", "csr"], writes=[f"t2b{i}"])
                tt("dve", dstT[:, 2 * h, :T], t1[:, 0, :T], t2[:, 1, :T], ALU.subtract,
                   reads=[f"t1b{i}", f"t2b{i}"], writes=[f"{dk}{2 * h}"])
                tt("dve", dstT[:, 2 * h + 1, :T], t2[:, 0, :T], t1[:, 1, :T], ALU.add,
                   reads=[f"t1b{i}", f"t2b{i}"], writes=[f"{dk}{2 * h + 1}"])
            for qk in range(2):
                dec = qdec if qk == 0 else kdec
                deck = "qdec" if qk == 0 else "kdec"
                dstT = qt if qk == 0 else kt
                dk = "q" if qk == 0 else "k"
                pendq = []
                for h in range(4):
                    bank, bk = ps_next()
                    for half in range(2):
                        for kc in range(8):
                            w, wk = wnext()
                            mm(bank[:, half * T:(half + 1) * T], w, uT[:, kc, :T], kc == 0, kc == 7, reads=[wk, f"u{kc}"], writes=[bk])
                    pendq.append((h, bank, bk))
                    if r is None and h == 0:
                        continue
                    if r is None:
                        r, rk = sumsq_rstd(T)
                        rt, rtk = rstd_tm(T)
                        tt("pool", csr[:, :, :T], cst[:, :, :T], bc2(r[:, :T], T), ALU.mult, reads=[csk, rk], writes=["csr"])
                    for (h, bank, bk) in pendq:
                        rot_evac(h, bank, bk, dec, deck, dstT, dk)
                    pendq = []
            for eb in range(4):
                vb = [ps_next() for _ in range(nb)]
                for kc in range(8):
                    w, wk = wnext(4)
                    for tb in range(nb):
                        mm(vb[tb][0][:rows, :512], uT[:, kc, tb * 128:tb * 128 + rows], w, kc == 0, kc == 7,
                           reads=[wk, f"u{kc}"], writes=[vb[tb][1]])
                for tb in range(nb):
                    act(AF.Copy, vt[:rows, tb, eb * 512:(eb + 1) * 512], vb[tb][0][:rows, :512], reads=[rtk], writes=[vb[tb][1], f"v{tb}"],
                        scale=rt[:rows, tb:tb + 1])
            for cp_ in range(8):
                bank, bk = ps_next()
                for half in range(2):
                    for kc in range(8):
                        w, wk = wnext()
                        mm(bank[:, half * T:(half + 1) * T], w, uT[:, kc, :T], kc == 0, kc == 7, reads=[wk, f"u{kc}"], writes=[bk])
                ia = st["abr"]
                st["abr"] = (ia + 1) % 2
                ab = abr[ia]
                tt("dve", ab[:, :, :T], b3(bank[:, :2 * T], T), bc2(r[:, :T], T), ALU.mult, reads=[rk], writes=[bk, f"abr{ia}"])
                act(AF.Silu, hT[:, 2 * cp_:2 * cp_ + 2, :T], ab[:, :, :T], reads=[f"abr{ia}"], writes=[f"h{2 * cp_}", f"h{2 * cp_ + 1}"])
            for jb in range(nb):
                for c in range(8):
                    tr(pbf[:rows, c * 128:(c + 1) * 128], kt[:, c, jb * 128:jb * 128 + rows], ident_b[:, :],
                       reads=[f"k{c}", "ident_b"], writes=["pbf"], sig=(c == 7))
                for h in range(4):
                    act(AF.Copy, ktm[:rows, jb, h * 256:(h + 1) * 256], pbf[:rows, h * 256:(h + 1) * 256],
                        reads=[], writes=["pbf", f"ktm{jb}"], scale=float(GAM[h] ** T))
            for h in range(4):
                bank, bk = ps_next()
                for jb in range(nb):
                    for dc in range(2):
                        mm(bank[:rows, jb * T:(jb + 1) * T], kt[:, 2 * h + dc, jb * 128:jb * 128 + rows], qt[:, 2 * h + dc, :T],
                           dc == 0, dc == 1, reads=[f"k{2 * h + dc}", f"q{2 * h + dc}"], writes=[bk])
                for jb in range(nb):
                    s0 = 128 - 128 * jb
                    tt("dve", smT[:rows, h, jb, :T], bank[:rows, jb * T:(jb + 1) * T], mask[:rows, s0:s0 + T], ALU.mult,
                       reads=["mask"], writes=[bk, f"sm{h}"])
            for h in range(4):
                oq = osq[h % 2]
                obanks = []
                for ep in range(2):
                    bank, bk = ps_next()
                    obanks.append((bank, bk))
                    for ei in range(2):
                        ec = ep * 2 + ei
                        for jb in range(nb):
                            mm(bank[:, ei * T:(ei + 1) * T], vt[:rows, jb, h * 512 + ec * 128:h * 512 + (ec + 1) * 128],
                               smT[:rows, h, jb, :T], jb == 0, False, reads=[f"v{jb}", f"sm{h}"], writes=[bk])
                        for dc in range(2):
                            mm(bank[:, ei * T:(ei + 1) * T], Sbf[:, 2 * h + dc, ec * 128:(ec + 1) * 128], qt[:, 2 * h + dc, :T],
                               False, dc == 1, reads=[f"Sbf{2 * h + dc}", f"q{2 * h + dc}"], writes=[bk])
                    act(AF.Square, oq[:, 2 * ep:2 * ep + 2, :T], b3(bank[:, :2 * T], T), reads=[], writes=[bk, f"osq{h % 2}_{ep}"])
                bank_s, bks = ps_next()
                for ec in range(4):
                    mm(bank_s[:, :T], ones_b[:, :], oq[:, ec, :T], ec == 0, ec == 3, reads=["ones_b", f"osq{h % 2}_{ec // 2}"], writes=[bks])
                i = st["rsh"]
                st["rsh"] = (i + 1) % 2
                r = rsh[i]
                rk = f"rsh{i}"
                rstd_from(bank_s, bks, T, r, rk, 1.0 / 512)
                for ec in range(4):
                    c = 4 * h + ec
                    tt("pool", hT[:, c, :T], hT[:, c, :T], r[:, :T], ALU.mult, reads=[rk], writes=[f"h{c}"])
                    bank, bk = obanks[ec // 2]
                    stt("dve", hT[:, c, :T], bank[:, (ec % 2) * T:(ec % 2 + 1) * T], rgn[:, j * 16 + c:j * 16 + c + 1], hT[:, c, :T],
                        ALU.mult, ALU.mult, reads=["rgn"], writes=[bk, f"h{c}"])
            for mp in range(4):
                bank, bk = ps_next()
                for mi in range(2):
                    for kc in range(16):
                        w, wk = wnext()
                        mm(bank[:, mi * T:(mi + 1) * T], w, hT[:, kc, :T], kc == 0, kc == 15, reads=[wk, f"h{kc}"], writes=[bk])
                xs_ = xT[:, 2 * mp:2 * mp + 2, :T]
                tt("dve", xs_, b3(bank[:, :2 * T], T), xs_, ALU.add, reads=[], writes=[bk, f"x{2 * mp}", f"x{2 * mp + 1}"])
                emit_xg([2 * mp, 2 * mp + 1], next_g, T)
            for h in range(4):
                for dc in range(2):
                    bank, bk = ps_next()
                    for jb in range(nb):
                        mm(bank[:, :512], ktm[:rows, jb, h * 256 + dc * 128:h * 256 + (dc + 1) * 128], vt[:rows, jb, h * 512:(h + 1) * 512],
                           jb == 0, jb == nb - 1, reads=[f"ktm{jb}", f"v{jb}"], writes=[bk])
                    i = 2 * h + dc
                    stt("dve", Sj[:, i, :], Sj[:, i, :], float(GAM[h] ** T), bank[:, :512], ALU.mult, ALU.add,
                        reads=[], writes=[bk, f"S{j}_{i}"])
            if last:
                dst = (nrp_d if stream == "p" else nrs_d)[j].rearrange("h (dc p) e -> p (h dc) e", p=128)
                dma("sp", dst, Sj[:, :, :], reads=[f"S{j}_{i}" for i in range(8)], writes=[], sem=f"so{j}")
            if not (last and l == 3):
                st["pending_sbf"] = [1 - j, 0]
                st["sbf"] = 1 - j
            else:
                st["sbf"] = None

        def load_x(src, T):
            rows = min(T, 128)
            nb = (T + 127) // 128
            for tb in range(nb):
                dma("sp", xin[:rows, tb, :], src[tb * 128:tb * 128 + rows, :], reads=[], writes=["xin"], sem="xi")

        def load_cs(buf, col0, T):
            dma("sp", cs[buf][:, 0, :T], cos_d[:, col0:col0 + T], reads=[], writes=[f"cs{buf}"], sem=f"csl{buf}")
            dma("sp", cs[buf][:, 1, :T], sin_d[:, col0:col0 + T], reads=[], writes=[f"cs{buf}"], sem=f"csl{buf}")

        def emit_input(T):
            rows = min(T, 128)
            nb = (T + 127) // 128
            for kp in range(4):
                bank, bk = ps_next()
                for ki in range(2):
                    kc = kp * 2 + ki
                    for tb in range(nb):
                        tr(bank[:, ki * T + tb * 128:ki * T + tb * 128 + rows], xin[:rows, tb, kc * 128:(kc + 1) * 128],
                           ident_f[:rows, :rows], reads=["xin", "ident_f"], writes=[bk], sig=(ki == 1 and tb == nb - 1))
                act(AF.Copy, xT[:, 2 * kp:2 * kp + 2, :T], b3(bank[:, :2 * T], T), reads=[], writes=[bk, f"x{2 * kp}", f"x{2 * kp + 1}"])
                emit_xg([2 * kp, 2 * kp + 1], 0, T)

        tiles = [("s", DEC_SEQ, NMETA + SEQ, xs_d, ys_d, True, True),
                 ("p", NMETA, 0, meta_d, None, True, False)]
        nt = SEQ // TM
        for i in range(nt):
            tiles.append(("p", TM, NMETA + i * TM, xp_d[i * TM:(i + 1) * TM, :], yp_d[i * TM:(i + 1) * TM, :], False, i == nt - 1))

        load_x(tiles[0][3], tiles[0][1])
        load_cs(0, tiles[0][2], tiles[0][1])

        for ti, (stream, T, col0, src, dst, first, last) in enumerate(tiles):
            rows = min(T, 128)
            nb = (T + 127) // 128
            csbuf = ti % 2
            if ti == 0:
                for j in range(2):
                    dma("sp", S[j][:, :, :], sret_d[j].rearrange("h (dc p) e -> p (h dc) e", p=128), reads=[],
                        writes=[f"S{j}_{i}" for i in range(8)], sem="sl")
                    for kc in range(8):
                        dma("sp", hist[j][:, kc, :], cpool_d[j][:, kc * 128:(kc + 1) * 128].rearrange("t p -> p t"), reads=[],
                            writes=[f"hist{j}"], sem="hl", slow=True)
                for j in range(2):
                    for i in range(8):
                        R._st(f"S{j}_{i}")["w"] = ("sl", R.cnt["sl"], 0)
                    R._st(f"hist{j}")["w"] = ("hl", R.cnt["hl"], 0)
            if ti == 1:
                for j in range(2):
                    for i in range(8):
                        R.op("pool", lambda e, o=S[j][:, i, :]: e.memset(o, 0.0), reads=[], writes=[f"S{j}_{i}"])
                    R.op("pool", lambda e, o=hist[j][:, :, :]: e.memset(o, 0.0), reads=[], writes=[f"hist{j}"])
                st["sbf"] = None
            if ti == 0:
                emit_input(T)
            for l in range(4):
                g2 = (l * 3 + 2) * 8
                gnext = (l + 1) * 3 * 8 if l < 3 else (96 if dst is not None else None)
                ffn(l, 0, T, (l * 3 + 1) * 8 if l % 2 == 1 else None)
                if l % 2 == 0:
                    pool_layer(l, T, first and stream == "p", last, stream, g2)
                else:
                    ret_layer(l, T, csbuf, last, stream, g2)
                ffn(l, 1, T, gnext)
                if l == 0 and ti + 1 < len(tiles):
                    nxt = tiles[ti + 1]
                    load_x(nxt[3], nxt[1])
                    load_cs((ti + 1) % 2, nxt[2], nxt[1])
            assert st["wpos"] == (ti + 1) * U_TILE
            if dst is not None:
                squares(T)
            if ti + 1 < len(tiles):
                emit_input(tiles[ti + 1][1])
            if dst is not None:
                rt, rtk = rstd_tm(T)
                for tb in range(nb):
                    for half in range(2):
                        bank, bk = ps_next()
                        for k4 in range(4):
                            kc = half * 4 + k4
                            tr(bank[:rows, k4 * 128:(k4 + 1) * 128], uf[:, kc, 15 + tb * 128:15 + tb * 128 + rows], ident_f[:, :],
                               reads=[f"uf{kc}", "ident_f"], writes=[bk], sig=(k4 == 3))
                        act(AF.Copy, yout[:rows, half * 512:(half + 1) * 512], bank[:rows, :512], reads=[rtk], writes=[bk, "yout"],
                            scale=rt[:rows, tb:tb + 1])
                    dma("act", dst[tb * 128:tb * 128 + rows, :], yout[:rows, :], reads=["yout"], writes=[], sem="yo0")

        R.final_waits = {k: R.cnt[k] for k in ["yo0", "yo1", "so0", "so1", "ho0", "ho1"] if k in R.cnt}

        with nc.Block() as block:
            def replay(engname):
                def body(e):
                    for waits, fn, inc in R.ops[engname]:
                        for sname, val in waits:
                            e.wait_ge(sems[sname], val)
                        ins = fn(e)
                        if inc is not None:
                            ins.then_inc(sems[inc[0]], inc[1])
                    if engname == "sp":
                        for sname, val in R.final_waits.items():
                            e.wait_ge(sems[sname], val)
                return body

            block.tensor(replay("pe"))
            block.scalar(replay("act"))
            block.vector(replay("dve"))
            block.gpsimd(replay("pool"))
            block.sync(replay("sp"))
    return nc


def _host_consts():
    half = 128
    inv = (10000.0 ** (-np.arange(half, dtype=np.float32) / np.float32(half))).astype(np.float32)
    pos = np.concatenate([np.arange(NMETA + SEQ), NMETA + PAST + np.arange(DEC_SEQ)]).astype(np.float32)
    ang = (inv[:, None] * pos[None, :]).astype(np.float32)
    cos = np.cos(ang).astype(np.float32)
    sin = np.sin(ang).astype(np.float32)
    jj = np.arange(128)[:, None]
    cc = np.arange(384)[None, :]
    cmask = ((cc - 128) >= jj).astype(np.float32)
    i = np.arange(TM, dtype=np.float64)
    qd = np.stack([np.float64(g) ** (i + 1) for g in GAM])
    kd = np.stack([np.float64(g) ** (-(i + 1)) / 16.0 for g in GAM])
    qdec = np.broadcast_to(qd.reshape(1, 4 * TM), (128, 4 * TM)).astype(np.float32)
    kdec = np.broadcast_to(kd.reshape(1, 4 * TM), (128, 4 * TM)).astype(np.float32)
    t = np.arange(16)
    ic = np.stack([1.0 / np.minimum(t + 1, w) for w in WIN]).reshape(1, 64)
    invcnt = np.broadcast_to(ic, (128, 64)).astype(np.float32)
    return {
        "rope_cos": np.ascontiguousarray(cos), "rope_sin": np.ascontiguousarray(sin),
        "cmask": np.ascontiguousarray(cmask), "qdec": np.ascontiguousarray(qdec), "kdec": np.ascontiguousarray(kdec),
        "invcnt": np.ascontiguousarray(invcnt), "ident": np.eye(128, dtype=np.float32),
        "ones": np.ones((128, 128), dtype=np.float32),
    }


_NC_CACHE = {}


def kernel(x_prompt, x_sample, cache_pool, state_ret, meta_tokens, norm_g, final_norm_g,
           w_ffn1_in, w_ffn1_out, w_ffn2_in, w_ffn2_out, w_pool, pool_scale,
           w_ret_in, w_ret_out, ret_norm_g):
    f = lambda a: np.ascontiguousarray(np.asarray(a, dtype=np.float32))
    x_prompt, x_sample, cache_pool, state_ret = f(x_prompt), f(x_sample), f(cache_pool), f(state_ret)
    g_all = np.concatenate([f(norm_g).reshape(12, D), f(final_norm_g).reshape(1, D)], 0)
    gn = np.ascontiguousarray(g_all.reshape(13, 8, 128).transpose(2, 0, 1).reshape(128, 104))
    pscale = np.ascontiguousarray(f(pool_scale).reshape(2, 8, 128).transpose(2, 0, 1).reshape(128, 16))
    rgn = np.ascontiguousarray(f(ret_norm_g).reshape(2, 16, 128).transpose(2, 0, 1).reshape(128, 32))
    shared = {
        "meta": f(meta_tokens), "gn": gn, "pscale": pscale, "rgn": rgn,
        "w_ffn1_in": f(w_ffn1_in), "w_ffn1_out": f(w_ffn1_out), "w_ffn2_in": f(w_ffn2_in), "w_ffn2_out": f(w_ffn2_out),
        "w_pool": f(w_pool), "w_ret_in": f(w_ret_in), "w_ret_out": f(w_ret_out),
    }
    shared.update(_host_consts())
    if "nc" not in _NC_CACHE:
        _NC_CACHE["nc"] = build_program()
    nc = _NC_CACHE["nc"]
    in_maps = []
    for c in range(8):
        m = dict(shared)
        m["xp"] = x_prompt[c]
        m["xs"] = x_sample[c]
        m["cpool"] = np.ascontiguousarray(cache_pool[:, c])
        m["sret"] = np.ascontiguousarray(state_ret[:, c])
        in_maps.append(m)
    res = run_bass_kernel_spmd(nc, in_maps, core_ids=list(range(8)))
    rs_ = res.results
    y_prompt = np.stack([rs_[c]["yp"] for c in range(8)], 0)
    y_sample = np.stack([rs_[c]["ys"] for c in range(8)], 0)
    npp = np.stack([rs_[c]["npp"] for c in range(8)], 1)
    nps = np.stack([rs_[c]["nps"] for c in range(8)], 1)
    nrp = np.stack([rs_[c]["nrp"] for c in range(8)], 1)
    nrs = np.stack([rs_[c]["nrs"] for c in range(8)], 1)
    return (y_prompt.astype(np.float32), y_sample.astype(np.float32), npp.astype(np.float32),
            nps.astype(np.float32), nrp.astype(np.float32), nrs.astype(np.float32))
```

```python
import contextlib
import numpy as np
import concourse.bass as bass
import concourse.mybir as mybir
from concourse.bass_utils import run_bass_kernel_spmd

F32 = mybir.dt.float32
BF16 = mybir.dt.bfloat16
ALU = mybir.AluOpType
AF = mybir.ActivationFunctionType

D = 1024
SEQ = 8192
NMETA = 16
DEC_SEQ = 64
PAST = 1024
DFF = 2816
NJ = DFF // 128
EPS = 1e-6
TM = 256
WIN = (2, 4, 8, 16)
GAM = [1.0 - 2.0 ** (-5.0 - h) for h in range(4)]
SLOT_U = 40
NSLOT = 4
U_FFN_IN = 352
U_FFN_OUT = 176
U_POOL = 16
U_RET = 512
U_TILE = 4 * 2 * (U_FFN_IN + U_FFN_OUT) + 2 * U_POOL + 2 * U_RET
NCHUNK = U_TILE // SLOT_U
NPOS = NMETA + SEQ + DEC_SEQ

ENG_ATTR = {"pe": "tensor", "act": "scalar", "dve": "vector", "pool": "gpsimd", "sp": "sync"}


class Rec:
    def __init__(self):
        self.ops = {e: [] for e in ENG_ATTR}
        self.cnt = {}
        self.known = {e: {} for e in ENG_ATTR}
        self.buf = {}
        self.final_waits = {}

    def _st(self, k):
        s = self.buf.get(k)
        if s is None:
            s = {"w": None, "r": []}
            self.buf[k] = s
        return s

    def op(self, eng, fn, reads=(), writes=(), sig=True, dma_sem=None):
        waits = {}
        idx = len(self.ops[eng])

        def need(tok):
            sem, val, pidx = tok
            if sem == eng:
                if eng == "pe":
                    return
                if pidx < idx - 2:
                    return
            if self.known[eng].get(sem, 0) >= val:
                return
            if waits.get(sem, 0) < val:
                waits[sem] = val

        for k in reads:
            s = self._st(k)
            if s["w"] is not None:
                need(s["w"])
        for k in writes:
            s = self._st(k)
            if s["w"] is not None:
                need(s["w"])
            for t in s["r"]:
                need(t)
        for sem, val in waits.items():
            self.known[eng][sem] = val
        if dma_sem is not None:
            self.cnt[dma_sem] = self.cnt.get(dma_sem, 0) + 16
            tok = (dma_sem, self.cnt[dma_sem], idx)
            inc = (dma_sem, 16)
        elif sig:
            self.cnt[eng] = self.cnt.get(eng, 0) + 1
            tok = (eng, self.cnt[eng], idx)
            inc = (eng, 1)
        else:
            tok = (eng, self.cnt.get(eng, 0) + 1, idx)
            inc = None
        for k in reads:
            if k in writes:
                continue
            self._st(k)["r"].append(tok)
        for k in writes:
            s = self._st(k)
            s["w"] = tok
            s["r"] = []
        self.ops[eng].append((list(waits.items()), fn, inc))
        return tok


def build_program():
    nc = bass.Bass("TRN2", target_bir_lowering=False)
    R = Rec()

    def din(name, shape, dt=F32):
        return nc.dram_tensor(name, list(shape), dt, kind="ExternalInput").ap()

    def dout(name, shape):
        return nc.dram_tensor(name, list(shape), F32, kind="ExternalOutput").ap()

    xp_d = din("xp", [SEQ, D])
    xs_d = din("xs", [DEC_SEQ, D])
    cpool_d = din("cpool", [2, 15, D])
    sret_d = din("sret", [2, 4, 256, 512])
    meta_d = din("meta", [NMETA, D])
    gn_d = din("gn", [128, 104])
    pscale_d = din("pscale", [128, 16])
    rgn_d = din("rgn", [128, 32])
    wf_in_d = [din("w_ffn1_in", [4, D, 2 * DFF]), din("w_ffn2_in", [4, D, 2 * DFF])]
    wf_out_d = [din("w_ffn1_out", [4, DFF, D]), din("w_ffn2_out", [4, DFF, D])]
    wpool_d = din("w_pool", [2, 4, 256, 256])
    wri_d = din("w_ret_in", [2, D, 6144])
    wro_d = din("w_ret_out", [2, 2048, D])
    cos_d = din("rope_cos", [128, NPOS])
    sin_d = din("rope_sin", [128, NPOS])
    mask_d = din("cmask", [128, 384])
    qdec_d = din("qdec", [128, 4 * TM])
    kdec_d = din("kdec", [128, 4 * TM])
    invc_d = din("invcnt", [128, 64])
    ident_d = din("ident", [128, 128])
    ones_d = din("ones", [128, 128])

    yp_d = dout("yp", [SEQ, D])
    ys_d = dout("ys", [DEC_SEQ, D])
    npp_d = dout("npp", [2, 15, D])
    nps_d = dout("nps", [2, 15, D])
    nrp_d = dout("nrp", [2, 4, 256, 512])
    nrs_d = dout("nrs", [2, 4, 256, 512])

    wstream = nc.dram_tensor("wstream", [128, U_TILE * 128], BF16, kind="Internal").ap()

    es = contextlib.ExitStack()
    with es:
        def sb(name, shape, dt=F32):
            return es.enter_context(nc.sbuf_tensor(name, list(shape), dt))

        def pst(name, shape, dt=F32):
            return es.enter_context(nc.psum_tensor(name, list(shape), dt))

        xT = sb("xT", [128, 8, TM])
        S = [sb("S0", [128, 8, 512]), sb("S1", [128, 8, 512])]
        Sbf = sb("Sbf", [128, 8, 512], BF16)
        hist = [sb("hist0", [128, 8, 15]), sb("hist1", [128, 8, 15])]
        ident_f = sb("ident_f", [128, 128])
        ident_b = sb("ident_b", [128, 128], BF16)
        ones_b = sb("ones_b", [128, 128], BF16)
        mask = sb("mask", [128, 384])
        qdec = sb("qdec_s", [128, 4, TM])
        kdec = sb("kdec_s", [128, 4, TM])
        invc = sb("invc", [128, 4, 16])
        gn = sb("gn_s", [128, 104])
        pscale = sb("pscale_s", [128, 16])
        rgn = sb("rgn_s", [128, 32])
        uT = sb("uT", [128, 8, TM], BF16)
        hT = sb("hT", [128, NJ, TM], BF16)
        sa = [sb(f"sa{i}", [128, TM]) for i in range(2)]
        abr = [sb(f"abr{i}", [128, 2, TM]) for i in range(2)]
        negh = sb("negh", [128, TM])
        csr = sb("csr", [128, 2, TM])
        rtm = [sb(f"rtm{i}", [128, 2]) for i in range(2)]
        dsq = sb("dsq", [128, 1])
        rs = [sb(f"rs{i}", [128, TM]) for i in range(2)]
        uf = sb("uf", [128, 8, 15 + TM])
        tmpw = [sb(f"tmpw{i}", [128, 2, 15 + TM]) for i in range(2)]
        qt = sb("qt", [128, 8, TM], BF16)
        kt = sb("kt", [128, 8, TM], BF16)
        ktm = sb("ktm", [128, 2, 1024], BF16)
        vt = sb("vt", [128, 2, 2048], BF16)
        smT = sb("smT", [128, 4, 2, TM], BF16)
        cs = [sb(f"cs{i}", [128, 2, TM]) for i in range(2)]
        xsb = [sb(f"xsb{i}", [128, 2, TM]) for i in range(2)]
        t1b = [sb(f"t1b{i}", [128, 2, TM]) for i in range(2)]
        t2b = [sb(f"t2b{i}", [128, 2, TM]) for i in range(2)]
        osq = [sb(f"osq{i}", [128, 4, TM], BF16) for i in range(2)]
        xsq = sb("xsq", [128, 8, TM], BF16)
        ntmp = [sb(f"ntmp{i}", [128, TM]) for i in range(2)]
        tmpd = [sb(f"tmpd{i}", [128, 2, 15 + TM]) for i in range(2)]
        rsh = [sb(f"rsh{i}", [128, TM]) for i in range(2)]
        xin = sb("xin", [128, 2, D])
        yout = sb("yout", [128, D])
        wsl = [sb(f"wsl{i}", [128, SLOT_U * 128], BF16) for i in range(NSLOT)]
        psb = [pst(f"ps{i}", [128, 512]) for i in range(7)]
        pbf = pst("pbf", [128, 1024], BF16)

        sem_names = list(ENG_ATTR) + ["cvt", "cst", "xi", "csl0", "csl1", "yo0", "yo1", "so0", "so1", "ho0", "ho1", "sl", "hl"] + [f"wl{i}" for i in range(NSLOT)]
        sems = {n: es.enter_context(nc.semaphore(n)) for n in sem_names}

        st = {"ps": 0, "sa": 0, "rs": 0, "xs": 0, "rsh": 0, "wpos": 0, "wload": 0, "nt": 0, "abr": 0, "rtm": 0}

        def ps_next():
            i = st["ps"]
            st["ps"] = (i + 1) % 7
            return psb[i], f"ps{i}"

        def mm(out, lhsT, rhs, start, stop, reads, writes, sig=None):
            if sig is None:
                sig = stop
            R.op("pe", lambda e, o=out, l=lhsT, r=rhs, s0=start, s1=stop: e.matmul(o, l, r, start=s0, stop=s1),
                 reads=reads, writes=writes, sig=sig)

        def tr(out, in_, ident, reads, writes, sig):
            R.op("pe", lambda e, o=out, i=in_, d=ident: e.transpose(o, i, d), reads=reads, writes=writes, sig=sig)

        def act(func, out, in_, reads, writes, scale=1.0, bias=0.0):
            R.op("act", lambda e, o=out, i=in_, f=func, s=scale, b=bias: e.activation(out=o, in_=i, func=f, bias=b, scale=s),
                 reads=reads, writes=writes)

        def tt(eng, out, in0, in1, op, reads, writes):
            R.op(eng, lambda e, o=out, a=in0, b=in1, p=op: e.tensor_tensor(o, a, b, p), reads=reads, writes=writes)

        def stt(eng, out, in0, scalar, in1, op0, op1, reads, writes):
            R.op(eng, lambda e, o=out, a=in0, s=scalar, b=in1, p0=op0, p1=op1: e.scalar_tensor_tensor(o, a, s, b, p0, p1),
                 reads=reads, writes=writes)

        def cp(eng, out, in_, reads, writes):
            R.op(eng, lambda e, o=out, i=in_: e.tensor_copy(o, i), reads=reads, writes=writes)

        def dma(eng, out, in_, reads, writes, sem, slow=False):
            if slow:
                fn = lambda e, o=out, i=in_: e.dma_start(out=o, in_=i, allow_slow_non_contiguous=True)
            else:
                fn = lambda e, o=out, i=in_: e.dma_start(out=o, in_=i)
            R.op(eng, fn, reads=reads, writes=writes, dma_sem=sem)

        def wload_upto(gchunk):
            while st["wload"] <= gchunk:
                g = st["wload"]
                c = g % NCHUNK
                s = g % NSLOT
                dma("sp", wsl[s][:, :], wstream[:, c * SLOT_U * 128:(c + 1) * SLOT_U * 128],
                    reads=["wstream"], writes=[f"w{s}"], sem=f"wl{s}")
                st["wload"] = g + 1

        def wnext(n=1):
            pos = st["wpos"]
            g = pos // SLOT_U
            off = pos % SLOT_U
            assert off + n <= SLOT_U
            wload_upto(g + NSLOT - 1)
            st["wpos"] = pos + n
            s = g % NSLOT
            return wsl[s][:, off * 128:(off + n) * 128], f"w{s}"

        def b3(ap2, T):
            return ap2.rearrange("p (a t) -> p a t", a=2)

        def bc2(ap, T):
            return ap.unsqueeze(1).broadcast_to([ap.shape[0], 2, T])

        def cdma(out, in_, key, eng="sp"):
            dma(eng, out, in_, reads=[], writes=[key], sem="cst")

        cdma(ident_f[:, :], ident_d[:, :], "ident_f")
        cdma(mask[:, :], mask_d[:, :], "mask")
        cdma(qdec[:, :, :], qdec_d.rearrange("p (h t) -> p h t", h=4), "qdec")
        cdma(kdec[:, :, :], kdec_d.rearrange("p (h t) -> p h t", h=4), "kdec")
        cdma(invc[:, :, :], invc_d.rearrange("p (g t) -> p g t", g=4), "invc")
        cdma(gn[:, :], gn_d[:, :], "gn")
        cdma(pscale[:, :], pscale_d[:, :], "pscale")
        cdma(rgn[:, :], rgn_d[:, :], "rgn")
        for k in ["ident_f", "mask", "qdec", "kdec", "invc", "gn", "pscale", "rgn"]:
            R._st(k)["w"] = ("cst", R.cnt["cst"], 0)
        R.op("pool", lambda e, o=negh[:, :]: e.memset(o, -0.5), reads=[], writes=["negh"])
        ncvt = [0]

        def cvt(out, in_):
            R.ops["pool"].append(([], (lambda e, o=out, i=in_: e.dma_start(out=o, in_=i)), ("cvt", 16)))
            ncvt[0] += 1

        cvt(ident_b[:, :], ident_d[:, :])
        cvt(ones_b[:, :], ones_d[:, :])
        upos = 0

        def wsview(u0, n):
            return wstream[:, u0 * 128:(u0 + n) * 128]

        def cvt_cols(w2d, c0, nkc):
            nonlocal upos
            src = w2d[:, c0:c0 + 128].rearrange("(kc p) c -> p kc c", p=128)
            cvt(wsview(upos, nkc).rearrange("p (kc c) -> p kc c", kc=nkc), src)
            upos += nkc

        for l in range(4):
            for which in range(2):
                if which == 1:
                    j = l // 2
                    if l % 2 == 0:
                        for g in range(4):
                            for dch in range(2):
                                cvt_cols(wpool_d[j, g], dch * 128, 2)
                    else:
                        w = wri_d[j]
                        for oc in list(range(8, 16)) + list(range(8)):
                            cvt_cols(w, oc * 128, 8)
                        for eb in range(4):
                            c0 = 2048 + eb * 512
                            src = w[:, c0:c0 + 512].rearrange("(kc p) c -> p kc c", p=128)
                            cvt(wsview(upos, 32).rearrange("p (kc c) -> p kc c", kc=8), src)
                            upos += 32
                        for oc in range(16):
                            cvt_cols(w, 4096 + oc * 128, 8)
                        wo = wro_d[j]
                        for m in range(8):
                            cvt_cols(wo, m * 128, 16)
                wi = wf_in_d[which][l]
                for j2 in range(NJ):
                    for part in range(2):
                        cvt_cols(wi, part * DFF + j2 * 128, 8)
                wo = wf_out_d[which][l]
                for m in range(8):
                    cvt_cols(wo, m * 128, NJ)
        assert upos == U_TILE, upos
        R.cnt["cvt"] = 16 * ncvt[0]
        tokc = ("cvt", R.cnt["cvt"], 0)
        for k in ["wstream", "ident_b", "ones_b"]:
            R._st(k)["w"] = tokc

        def squares(T, no_act=False):
            for kp in range(4):
                xs2 = xT[:, 2 * kp:2 * kp + 2, :T]
                if kp % 2 == 1 and no_act:
                    tt("dve", xsq[:, 2 * kp:2 * kp + 2, :T], xs2, xs2, ALU.mult,
                       reads=[f"x{2 * kp}", f"x{2 * kp + 1}"], writes=[f"xq{2 * kp}", f"xq{2 * kp + 1}"])
                elif kp % 2 == 1:
                    act(AF.Square, xsq[:, 2 * kp:2 * kp + 2, :T], xs2, reads=[f"x{2 * kp}", f"x{2 * kp + 1}"],
                        writes=[f"xq{2 * kp}", f"xq{2 * kp + 1}"])
                else:
                    tt("pool", xsq[:, 2 * kp:2 * kp + 2, :T], xs2, xs2, ALU.mult,
                       reads=[f"x{2 * kp}", f"x{2 * kp + 1}"], writes=[f"xq{2 * kp}", f"xq{2 * kp + 1}"])

        def rstd_from(bank, bk, T, rbuf, rk, inv_n):
            act(AF.Sqrt, rbuf[:, :T], bank[:, :T], reads=[], writes=[bk, rk], scale=inv_n, bias=EPS)
            R.op("dve", lambda e, o=rbuf[:, :T]: e.reciprocal(o, o), reads=[], writes=[rk])

        def sumsq_rstd(T):
            bank, bk = ps_next()
            for kc in range(8):
                mm(bank[:, :T], ones_b[:, :], xsq[:, kc, :T], kc == 0, kc == 7, reads=["ones_b", f"xq{kc}"], writes=[bk])
            i = st["rs"]
            st["rs"] = (i + 1) % 2
            rstd_from(bank, bk, T, rs[i], f"rs{i}", 1.0 / D)
            return rs[i], f"rs{i}"

        def rmsnorm(gcol, T, dst):
            act(AF.Sqrt, dsq[:, 0:1], gn[:, 0:1], reads=["gn"], writes=["dsq"])
            squares(T)
            r, rk = sumsq_rstd(T)
            for kc in range(8):
                if dst == "u":
                    o = uT[:, kc, :T]
                    wk = f"u{kc}"
                else:
                    o = uf[:, kc, 15:15 + T]
                    wk = f"uf{kc}"
                gcolap = gn[:, gcol + kc:gcol + kc + 1]
                if kc % 2 == 0:
                    stt("dve", o, xT[:, kc, :T], gcolap, r[:, :T], ALU.mult, ALU.mult,
                        reads=[f"x{kc}", rk, "gn"], writes=[wk])
                else:
                    ti_ = st["nt"]
                    st["nt"] = (ti_ + 1) % 2
                    tt("pool", ntmp[ti_][:, :T], xT[:, kc, :T], r[:, :T], ALU.mult, reads=[f"x{kc}", rk], writes=[f"ntmp{ti_}"])
                    act(AF.Copy, o, ntmp[ti_][:, :T], reads=[f"ntmp{ti_}", "gn"], writes=[wk], scale=gcolap)

        def emit_xg(chunks, gcol, T):
            if gcol is None:
                return
            for kc in chunks:
                gcolap = gn[:, gcol + kc:gcol + kc + 1]
                if gcol == 96:
                    o, wk = uf[:, kc, 15:15 + T], f"uf{kc}"
                else:
                    o, wk = uT[:, kc, :T], f"u{kc}"
                if kc % 2 == 0:
                    R.op("dve", lambda e, o=o, a_=xT[:, kc, :T], g_=gcolap: e.tensor_scalar(o, a_, g_, None, ALU.mult),
                         reads=[f"x{kc}", "gn"], writes=[wk])
                else:
                    act(AF.Copy, o, xT[:, kc, :T], reads=[f"x{kc}", "gn"], writes=[wk], scale=gcolap)

        def rstd_tm(T):
            rows = min(T, 128)
            nb = (T + 127) // 128
            bank, bk = ps_next()
            for tb in range(nb):
                for kc in range(8):
                    mm(bank[:rows, tb:tb + 1], xsq[:, kc, tb * 128:tb * 128 + rows], ones_b[:, 0:1], kc == 0, kc == 7,
                       reads=["ones_b", f"xq{kc}"], writes=[bk])
            i = st["rtm"]
            st["rtm"] = (i + 1) % 2
            rt = rtm[i]
            act(AF.Sqrt, rt[:rows, :nb], bank[:rows, :nb], reads=[], writes=[bk, f"rtm{i}"], scale=1.0 / D, bias=EPS)
            R.op("dve", lambda e, o=rt[:rows, :nb]: e.reciprocal(o, o), reads=[], writes=[f"rtm{i}"])
            return rt, f"rtm{i}"

        def ffn(l, which, T, next_g):
            squares(T, no_act=(l == 0 and which == 0))
            r = rk = None
            pend = []
            for j in range(NJ):
                bank, bk = ps_next()
                for part in range(2):
                    for kc in range(8):
                        w, wk = wnext()
                        mm(bank[:, part * T:(part + 1) * T], w, uT[:, kc, :T], kc == 0, kc == 7, reads=[wk, f"u{kc}"], writes=[bk])
                pend.append((j, bank, bk))
                if j == 0:
                    continue
                if j == 1:
                    r, rk = sumsq_rstd(T)
                for (jj, bank_, bk_) in pend:
                    ia = st["abr"]
                    st["abr"] = (ia + 1) % 2
                    ab = abr[ia]
                    tt("dve", ab[:, :, :T], b3(bank_[:, :2 * T], T), bc2(r[:, :T], T), ALU.mult, reads=[rk], writes=[bk_, f"abr{ia}"])
                    i = st["sa"]
                    st["sa"] = (i + 1) % 2
                    act(AF.Silu, sa[i][:, :T], ab[:, 0, :T], reads=[f"abr{ia}"], writes=[f"sa{i}"])
                    tt("pool", hT[:, jj, :T], sa[i][:, :T], ab[:, 1, :T], ALU.mult, reads=[f"sa{i}", f"abr{ia}"], writes=[f"h{jj}"])
                pend = []
                ps_ = st.get("pending_sbf")
                if ps_ is not None and j >= 2:
                    jo, ci = ps_
                    act(AF.Copy, Sbf[:, ci, :], S[jo][:, ci, :], reads=[f"S{jo}_{ci}"], writes=[f"Sbf{ci}"])
                    st["pending_sbf"] = [jo, ci + 1] if ci < 7 else None
            for mp in range(4):
                bank, bk = ps_next()
                for mi in range(2):
                    for kc in range(NJ):
                        w, wk = wnext()
                        mm(bank[:, mi * T:(mi + 1) * T], w, hT[:, kc, :T], kc == 0, kc == NJ - 1, reads=[wk, f"h{kc}"], writes=[bk])
                xs_ = xT[:, 2 * mp:2 * mp + 2, :T]
                stt("dve", xs_, b3(bank[:, :2 * T], T), 0.5, xs_, ALU.mult, ALU.add,
                    reads=[], writes=[bk, f"x{2 * mp}", f"x{2 * mp + 1}"])
                emit_xg([2 * mp, 2 * mp + 1], next_g, T)

        def pool_layer(l, T, first_prompt, last, stream, next_g):
            j = l // 2
            L = 15 + T
            ufk = [f"uf{k}" for k in range(8)]
            cp("pool", uf[:, :, 0:15], hist[j][:, :, :], reads=[f"hist{j}"], writes=ufk)
            rmsnorm((l * 3 + 1) * 8, T, "uf")
            for g in (0, 1, 2, 3):
                c0 = 2 * g
                weng = "pool" if g == 3 else "dve"
                tbufs = tmpw if g == 3 else tmpd
                tname = "tmpw" if g == 3 else "tmpd"
                cur = uf[:, c0:c0 + 2, :]
                curk = [f"uf{c0}", f"uf{c0 + 1}"]
                for d in range(g + 1):
                    s = 1 << d
                    v0 = (1 << (d + 1)) - 1
                    dstt = tbufs[d % 2]
                    tt(weng, dstt[:, :, v0:L], cur[:, :, v0:L], cur[:, :, v0 - s:L - s], ALU.add,
                       reads=curk, writes=[f"{tname}{d % 2}"])
                    cur = dstt
                    curk = [f"{tname}{d % 2}"]
                o = uT[:, c0:c0 + 2, :T]
                ok = [f"u{c0}", f"u{c0 + 1}"]
                if first_prompt:
                    other = tbufs[(g + 1) % 2]
                    otherk = f"{tname}{(g + 1) % 2}"
                    tt(weng, other[:, :, 15:15 + T], cur[:, :, 15:15 + T], bc2(invc[:, g, :T], T), ALU.mult,
                       reads=curk + ["invc"], writes=[otherk])
                    tt(weng, o, other[:, :, 15:15 + T], uf[:, c0:c0 + 2, 15:15 + T], ALU.subtract,
                       reads=[otherk, f"uf{c0}", f"uf{c0 + 1}"], writes=ok)
                else:
                    stt("dve", o, cur[:, :, 15:15 + T], 1.0 / WIN[g], uf[:, c0:c0 + 2, 15:15 + T], ALU.mult, ALU.subtract,
                        reads=curk + [f"uf{c0}", f"uf{c0 + 1}"], writes=ok)
            for g in range(4):
                bank, bk = ps_next()
                for dch in range(2):
                    for cc in range(2):
                        w, wk = wnext()
                        mm(bank[:, dch * T:(dch + 1) * T], w, uT[:, 2 * g + cc, :T], cc == 0, cc == 1, reads=[wk, f"u{2 * g + cc}"], writes=[bk])
                for dch in range(2):
                    c = 2 * g + dch
                    stt("dve", xT[:, c, :T], bank[:, dch * T:(dch + 1) * T], pscale[:, j * 8 + c:j * 8 + c + 1], xT[:, c, :T],
                        ALU.mult, ALU.add, reads=["pscale"], writes=[bk, f"x{c}"])
                emit_xg([2 * g, 2 * g + 1], next_g, T)
            cp("pool", hist[j][:, :, :], uf[:, :, T:T + 15], reads=ufk, writes=[f"hist{j}"])
            if last:
                dstd = (npp_d if stream == "p" else nps_d)[j]
                for kc in range(8):
                    dma("sp", dstd[:, kc * 128:(kc + 1) * 128].rearrange("t p -> p t"), hist[j][:, kc, :],
                        reads=[f"hist{j}"], writes=[], sem=f"ho{j}", slow=True)

        def ret_layer(l, T, csbuf, last, stream, next_g):
            j = l // 2
            Sj = S[j]
            rows = min(T, 128)
            nb = (T + 127) // 128
            if st.get("sbf") != j:
                for i in range(8):
                    act(AF.Copy, Sbf[:, i, :], Sj[:, i, :], reads=[f"S{j}_{i}"], writes=[f"Sbf{i}"])
                st["sbf"] = j
            squares(T)
            cst = cs[csbuf]
            csk = f"cs{csbuf}"
            r = rk = rt = rtk = None

            def rot_evac(h, bank, bk, dec, deck, dstT, dk):
                i = st["xs"]
                st["xs"] = (i + 1) % 2
                xs_, t1, t2 = xsb[i], t1b[i], t2b[i]
                tt("dve", xs_[:, :, :T], b3(bank[:, :2 * T], T), bc2(dec[:, h, :T], T), ALU.mult,
                   reads=[deck], writes=[bk, f"xsb{i}"])
                tt("pool", t1[:, :, :T], xs_[:, :, :T], bc2(csr[:, 0, :T], T), ALU.mult, reads=[f"xsb{i}", "csr"], writes=[f"t1b{i}"])
                tt("pool" if h % 2 == 0 else "dve", t2[:, :, :T], xs_[:, :, :T], bc2(csr[:, 1, :T], T), ALU.mult,
                   reads=[f"xsb{i}", "csr"], writes=[f"t2b{i}"])
                tt("dve", dstT[:, 2 * h, :T], t1[:, 0, :T], t2[:, 1, :T], ALU.subtract,
                   reads=[f"t1b{i}", f"t2b{i}"], writes=[f"{dk}{2 * h}"])
                tt("dve", dstT[:, 2 * h + 1, :T], t2[:, 0, :T], t1[:, 1, :T], ALU.add,
                   reads=[f"t1b{i}", f"t2b{i}"], writes=[f"{dk}{2 * h + 1}"])
            for qk in (1, 0):
                dec = qdec if qk == 0 else kdec
                deck = "qdec" if qk == 0 else "kdec"
                dstT = qt if qk == 0 else kt
                dk = "q" if qk == 0 else "k"
                pendq = []
                for h in range(4):
                    bank, bk = ps_next()
                    for half in range(2):
                        for kc in range(8):
                            w, wk = wnext()
                            mm(bank[:, half * T:(half + 1) * T], w, uT[:, kc, :T], kc == 0, kc == 7, reads=[wk, f"u{kc}"], writes=[bk])
                    pendq.append((h, bank, bk))
                    if r is None and h == 0:
                        continue
                    if r is None:
                        r, rk = sumsq_rstd(T)
                        rt, rtk = rstd_tm(T)
                        tt("pool", csr[:, :, :T], cst[:, :, :T], bc2(r[:, :T], T), ALU.mult, reads=[csk, rk], writes=["csr"])
                    for (h, bank, bk) in pendq:
                        rot_evac(h, bank, bk, dec, deck, dstT, dk)
                    pendq = []
            for eb in range(4):
                vb = [ps_next() for _ in range(nb)]
                for kc in range(8):
                    w, wk = wnext(4)
                    for tb in range(nb):
                        mm(vb[tb][0][:rows, :512], uT[:, kc, tb * 128:tb * 128 + rows], w, kc == 0, kc == 7,
                           reads=[wk, f"u{kc}"], writes=[vb[tb][1]])
                for tb in range(nb):
                    act(AF.Copy, vt[:rows, tb, eb * 512:(eb + 1) * 512], vb[tb][0][:rows, :512], reads=[rtk], writes=[vb[tb][1], f"v{tb}"],
                        scale=rt[:rows, tb:tb + 1])
            def k_tm(jb):
                for c in range(8):
                    tr(pbf[:rows, c * 128:(c + 1) * 128], kt[:, c, jb * 128:jb * 128 + rows], ident_b[:, :],
                       reads=[f"k{c}", "ident_b"], writes=["pbf"], sig=(c == 7))
                for h in range(4):
                    act(AF.Copy, ktm[:rows, jb, h * 256:(h + 1) * 256], pbf[:rows, h * 256:(h + 1) * 256],
                        reads=[], writes=["pbf", f"ktm{jb}"], scale=float(GAM[h] ** T))

            for cp_ in range(8):
                if cp_ % 4 == 0 and cp_ // 4 < nb:
                    k_tm(cp_ // 4)
                bank, bk = ps_next()
                for half in range(2):
                    for kc in range(8):
                        w, wk = wnext()
                        mm(bank[:, half * T:(half + 1) * T], w, uT[:, kc, :T], kc == 0, kc == 7, reads=[wk, f"u{kc}"], writes=[bk])
                ia = st["abr"]
                st["abr"] = (ia + 1) % 2
                ab = abr[ia]
                tt("dve", ab[:, :, :T], b3(bank[:, :2 * T], T), bc2(r[:, :T], T), ALU.mult, reads=[rk], writes=[bk, f"abr{ia}"])
                act(AF.Silu, hT[:, 2 * cp_:2 * cp_ + 2, :T], ab[:, :, :T], reads=[f"abr{ia}"], writes=[f"h{2 * cp_}", f"h{2 * cp_ + 1}"])
            for h in range(4):
                bank, bk = ps_next()
                for jb in range(nb):
                    for dc in range(2):
                        mm(bank[:rows, jb * T:(jb + 1) * T], kt[:, 2 * h + dc, jb * 128:jb * 128 + rows], qt[:, 2 * h + dc, :T],
                           dc == 0, dc == 1, reads=[f"k{2 * h + dc}", f"q{2 * h + dc}"], writes=[bk])
                for jb in range(nb):
                    s0 = 128 - 128 * jb
                    tt("dve", smT[:rows, h, jb, :T], bank[:rows, jb * T:(jb + 1) * T], mask[:rows, s0:s0 + T], ALU.mult,
                       reads=["mask"], writes=[bk, f"sm{h}"])
            for h in range(4):
                for dc in range(2):
                    bank, bk = ps_next()
                    for jb in range(nb):
                        mm(bank[:, :512], ktm[:rows, jb, h * 256 + dc * 128:h * 256 + (dc + 1) * 128], vt[:rows, jb, h * 512:(h + 1) * 512],
                           jb == 0, jb == nb - 1, reads=[f"ktm{jb}", f"v{jb}"], writes=[bk])
                    i = 2 * h + dc
                    stt("dve", Sj[:, i, :], Sj[:, i, :], float(GAM[h] ** T), bank[:, :512], ALU.mult, ALU.add,
                        reads=[], writes=[bk, f"S{j}_{i}"])
            for h in range(4):
                oq = osq[h % 2]
                obanks = []
                for ep in range(2):
                    bank, bk = ps_next()
                    obanks.append((bank, bk))
                    for ei in range(2):
                        ec = ep * 2 + ei
                        for jb in range(nb):
                            mm(bank[:, ei * T:(ei + 1) * T], vt[:rows, jb, h * 512 + ec * 128:h * 512 + (ec + 1) * 128],
                               smT[:rows, h, jb, :T], jb == 0, False, reads=[f"v{jb}", f"sm{h}"], writes=[bk])
                        for dc in range(2):
                            mm(bank[:, ei * T:(ei + 1) * T], Sbf[:, 2 * h + dc, ec * 128:(ec + 1) * 128], qt[:, 2 * h + dc, :T],
                               False, dc == 1, reads=[f"Sbf{2 * h + dc}", f"q{2 * h + dc}"], writes=[bk])
                    act(AF.Square, oq[:, 2 * ep:2 * ep + 2, :T], b3(bank[:, :2 * T], T), reads=[], writes=[bk, f"osq{h % 2}_{ep}"])
                bank_s, bks = ps_next()
                for ec in range(4):
                    mm(bank_s[:, :T], ones_b[:, :], oq[:, ec, :T], ec == 0, ec == 3, reads=["ones_b", f"osq{h % 2}_{ec // 2}"], writes=[bks])
                i = st["rsh"]
                st["rsh"] = (i + 1) % 2
                r = rsh[i]
                rk = f"rsh{i}"
                rstd_from(bank_s, bks, T, r, rk, 1.0 / 512)
                for ec in range(4):
                    c = 4 * h + ec
                    tt("pool", hT[:, c, :T], hT[:, c, :T], r[:, :T], ALU.mult, reads=[rk], writes=[f"h{c}"])
                    bank, bk = obanks[ec // 2]
                    stt("dve", hT[:, c, :T], bank[:, (ec % 2) * T:(ec % 2 + 1) * T], rgn[:, j * 16 + c:j * 16 + c + 1], hT[:, c, :T],
                        ALU.mult, ALU.mult, reads=["rgn"], writes=[bk, f"h{c}"])
            for mp in range(4):
                bank, bk = ps_next()
                for mi in range(2):
                    for kc in range(16):
                        w, wk = wnext()
                        mm(bank[:, mi * T:(mi + 1) * T], w, hT[:, kc, :T], kc == 0, kc == 15, reads=[wk, f"h{kc}"], writes=[bk])
                xs_ = xT[:, 2 * mp:2 * mp + 2, :T]
                tt("dve", xs_, b3(bank[:, :2 * T], T), xs_, ALU.add, reads=[], writes=[bk, f"x{2 * mp}", f"x{2 * mp + 1}"])
                emit_xg([2 * mp, 2 * mp + 1], next_g, T)
            if last:
                dst = (nrp_d if stream == "p" else nrs_d)[j].rearrange("h (dc p) e -> p (h dc) e", p=128)
                dma("sp", dst, Sj[:, :, :], reads=[f"S{j}_{i}" for i in range(8)], writes=[], sem=f"so{j}")
            if not (last and l == 3):
                st["pending_sbf"] = [1 - j, 0]
                st["sbf"] = 1 - j
            else:
                st["sbf"] = None

        def load_x(src, T):
            rows = min(T, 128)
            nb = (T + 127) // 128
            for tb in range(nb):
                dma("sp", xin[:rows, tb, :], src[tb * 128:tb * 128 + rows, :], reads=[], writes=["xin"], sem="xi")

        def load_cs(buf, col0, T):
            dma("sp", cs[buf][:, 0, :T], cos_d[:, col0:col0 + T], reads=[], writes=[f"cs{buf}"], sem=f"csl{buf}")
            dma("sp", cs[buf][:, 1, :T], sin_d[:, col0:col0 + T], reads=[], writes=[f"cs{buf}"], sem=f"csl{buf}")

        def emit_input(T):
            rows = min(T, 128)
            nb = (T + 127) // 128
            for kp in range(4):
                bank, bk = ps_next()
                for ki in range(2):
                    kc = kp * 2 + ki
                    for tb in range(nb):
                        tr(bank[:, ki * T + tb * 128:ki * T + tb * 128 + rows], xin[:rows, tb, kc * 128:(kc + 1) * 128],
                           ident_f[:rows, :rows], reads=["xin", "ident_f"], writes=[bk], sig=(ki == 1 and tb == nb - 1))
                act(AF.Copy, xT[:, 2 * kp:2 * kp + 2, :T], b3(bank[:, :2 * T], T), reads=[], writes=[bk, f"x{2 * kp}", f"x{2 * kp + 1}"])
                emit_xg([2 * kp, 2 * kp + 1], 0, T)

        tiles = [("s", DEC_SEQ, NMETA + SEQ, xs_d, ys_d, True, True),
                 ("p", NMETA, 0, meta_d, None, True, False)]
        nt = SEQ // TM
        for i in range(nt):
            tiles.append(("p", TM, NMETA + i * TM, xp_d[i * TM:(i + 1) * TM, :], yp_d[i * TM:(i + 1) * TM, :], False, i == nt - 1))

        load_x(tiles[0][3], tiles[0][1])
        load_cs(0, tiles[0][2], tiles[0][1])

        for ti, (stream, T, col0, src, dst, first, last) in enumerate(tiles):
            rows = min(T, 128)
            nb = (T + 127) // 128
            csbuf = ti % 2
            if ti == 0:
                for j in range(2):
                    dma("sp", S[j][:, :, :], sret_d[j].rearrange("h (dc p) e -> p (h dc) e", p=128), reads=[],
                        writes=[f"S{j}_{i}" for i in range(8)], sem="sl")
                    for kc in range(8):
                        dma("sp", hist[j][:, kc, :], cpool_d[j][:, kc * 128:(kc + 1) * 128].rearrange("t p -> p t"), reads=[],
                            writes=[f"hist{j}"], sem="hl", slow=True)
                for j in range(2):
                    for i in range(8):
                        R._st(f"S{j}_{i}")["w"] = ("sl", R.cnt["sl"], 0)
                    R._st(f"hist{j}")["w"] = ("hl", R.cnt["hl"], 0)
            if ti == 1:
                for j in range(2):
                    for i in range(8):
                        R.op("pool", lambda e, o=S[j][:, i, :]: e.memset(o, 0.0), reads=[], writes=[f"S{j}_{i}"])
                    R.op("pool", lambda e, o=hist[j][:, :, :]: e.memset(o, 0.0), reads=[], writes=[f"hist{j}"])
                st["sbf"] = None
            if ti == 0:
                emit_input(T)
            for l in range(4):
                g2 = (l * 3 + 2) * 8
                gnext = (l + 1) * 3 * 8 if l < 3 else (96 if dst is not None else None)
                ffn(l, 0, T, (l * 3 + 1) * 8 if l % 2 == 1 else None)
                if l % 2 == 0:
                    pool_layer(l, T, first and stream == "p", last, stream, g2)
                else:
                    ret_layer(l, T, csbuf, last, stream, g2)
                ffn(l, 1, T, gnext)
                if l == 0 and ti + 1 < len(tiles):
                    nxt = tiles[ti + 1]
                    load_x(nxt[3], nxt[1])
                    load_cs((ti + 1) % 2, nxt[2], nxt[1])
            assert st["wpos"] == (ti + 1) * U_TILE
            if dst is not None:
                squares(T)
            if ti + 1 < len(tiles):
                emit_input(tiles[ti + 1][1])
            if dst is not None:
                rt, rtk = rstd_tm(T)
                for tb in range(nb):
                    for half in range(2):
                        bank, bk = ps_next()
                        for k4 in range(4):
                            kc = half * 4 + k4
                            tr(bank[:rows, k4 * 128:(k4 + 1) * 128], uf[:, kc, 15 + tb * 128:15 + tb * 128 + rows], ident_f[:, :],
                               reads=[f"uf{kc}", "ident_f"], writes=[bk], sig=(k4 == 3))
                        act(AF.Copy, yout[:rows, half * 512:(half + 1) * 512], bank[:rows, :512], reads=[rtk], writes=[bk, "yout"],
                            scale=rt[:rows, tb:tb + 1])
                    dma("act", dst[tb * 128:tb * 128 + rows, :], yout[:rows, :], reads=["yout"], writes=[], sem="yo0")

        R.final_waits = {k: R.cnt[k] for k in ["yo0", "yo1", "so0", "so1", "ho0", "ho1"] if k in R.cnt}

        with nc.Block() as block:
            def replay(engname):
                def body(e):
                    for waits, fn, inc in R.ops[engname]:
                        for sname, val in waits:
                            e.wait_ge(sems[sname], val)
                        ins = fn(e)
                        if inc is not None:
                            ins.then_inc(sems[inc[0]], inc[1])
                    if engname == "sp":
                        for sname, val in R.final_waits.items():
                            e.wait_ge(sems[sname], val)
                return body

            block.tensor(replay("pe"))
            block.scalar(replay("act"))
            block.vector(replay("dve"))
            block.gpsimd(replay("pool"))
            block.sync(replay("sp"))
    return nc


def _host_consts():
    half = 128
    inv = (10000.0 ** (-np.arange(half, dtype=np.float32) / np.float32(half))).astype(np.float32)
    pos = np.concatenate([np.arange(NMETA + SEQ), NMETA + PAST + np.arange(DEC_SEQ)]).astype(np.float32)
    ang = (inv[:, None] * pos[None, :]).astype(np.float32)
    cos = np.cos(ang).astype(np.float32)
    sin = np.sin(ang).astype(np.float32)
    jj = np.arange(128)[:, None]
    cc = np.arange(384)[None, :]
    cmask = ((cc - 128) >= jj).astype(np.float32)
    i = np.arange(TM, dtype=np.float64)
    qd = np.stack([np.float64(g) ** (i + 1) for g in GAM])
    kd = np.stack([np.float64(g) ** (-(i + 1)) / 16.0 for g in GAM])
    qdec = np.broadcast_to(qd.reshape(1, 4 * TM), (128, 4 * TM)).astype(np.float32)
    kdec = np.broadcast_to(kd.reshape(1, 4 * TM), (128, 4 * TM)).astype(np.float32)
    t = np.arange(16)
    ic = np.stack([1.0 / np.minimum(t + 1, w) for w in WIN]).reshape(1, 64)
    invcnt = np.broadcast_to(ic, (128, 64)).astype(np.float32)
    return {
        "rope_cos": np.ascontiguousarray(cos), "rope_sin": np.ascontiguousarray(sin),
        "cmask": np.ascontiguousarray(cmask), "qdec": np.ascontiguousarray(qdec), "kdec": np.ascontiguousarray(kdec),
        "invcnt": np.ascontiguousarray(invcnt), "ident": np.eye(128, dtype=np.float32),
        "ones": np.ones((128, 128), dtype=np.float32),
    }


_NC_CACHE = {}


def kernel(x_prompt, x_sample, cache_pool, state_ret, meta_tokens, norm_g, final_norm_g,
           w_ffn1_in, w_ffn1_out, w_ffn2_in, w_ffn2_out, w_pool, pool_scale,
           w_ret_in, w_ret_out, ret_norm_g):
    f = lambda a: np.ascontiguousarray(np.asarray(a, dtype=np.float32))
    x_prompt, x_sample, cache_pool, state_ret = f(x_prompt), f(x_sample), f(cache_pool), f(state_ret)
    g_all = np.concatenate([f(norm_g).reshape(12, D), f(final_norm_g).reshape(1, D)], 0)
    gn = np.ascontiguousarray(g_all.reshape(13, 8, 128).transpose(2, 0, 1).reshape(128, 104))
    pscale = np.ascontiguousarray(f(pool_scale).reshape(2, 8, 128).transpose(2, 0, 1).reshape(128, 16))
    rgn = np.ascontiguousarray(f(ret_norm_g).reshape(2, 16, 128).transpose(2, 0, 1).reshape(128, 32))
    shared = {
        "meta": f(meta_tokens), "gn": gn, "pscale": pscale, "rgn": rgn,
        "w_ffn1_in": f(w_ffn1_in), "w_ffn1_out": f(w_ffn1_out), "w_ffn2_in": f(w_ffn2_in), "w_ffn2_out": f(w_ffn2_out),
        "w_pool": f(w_pool), "w_ret_in": f(w_ret_in), "w_ret_out": f(w_ret_out),
    }
    shared.update(_host_consts())
    if "nc" not in _NC_CACHE:
        _NC_CACHE["nc"] = build_program()
    nc = _NC_CACHE["nc"]
    in_maps = []
    for c in range(8):
        m = dict(shared)
        m["xp"] = x_prompt[c]
        m["xs"] = x_sample[c]
        m["cpool"] = np.ascontiguousarray(cache_pool[:, c])
        m["sret"] = np.ascontiguousarray(state_ret[:, c])
        in_maps.append(m)
    res = run_bass_kernel_spmd(nc, in_maps, core_ids=list(range(8)))
    rs_ = res.results
    y_prompt = np.stack([rs_[c]["yp"] for c in range(8)], 0)
    y_sample = np.stack([rs_[c]["ys"] for c in range(8)], 0)
    npp = np.stack([rs_[c]["npp"] for c in range(8)], 1)
    nps = np.stack([rs_[c]["nps"] for c in range(8)], 1)
    nrp = np.stack([rs_[c]["nrp"] for c in range(8)], 1)
    nrs = np.stack([rs_[c]["nrs"] for c in range(8)], 1)
    return (y_prompt.astype(np.float32), y_sample.astype(np.float32), npp.astype(np.float32),
            nps.astype(np.float32), nrp.astype(np.float32), nrs.astype(np.float32))
```
